# Optimizing a Trainium2 kernel written in Bass

```python
import math
import jax, jax.numpy as jnp
from jax import lax
import numpy as np

D_MODEL = 1024
BATCH = 2
SEQ = 16384
DEPTH = 2

HEAD_DIM = 64
SB_HEADS = 4
DIL_HEADS = 4
HGRN_HEADS = 4
HGRN_DK = 128
HGRN_DV = 128
SB_WIDTH = SB_HEADS * HEAD_DIM
DIL_WIDTH = DIL_HEADS * HEAD_DIM
HGRN_WIDTH = HGRN_HEADS * HGRN_DV
HGRN_KDIM = HGRN_HEADS * HGRN_DK
MIX_WIDTH = SB_WIDTH + DIL_WIDTH + HGRN_WIDTH
IN_SPLITS = (SB_WIDTH, SB_WIDTH, SB_WIDTH, DIL_WIDTH, DIL_WIDTH, DIL_WIDTH, HGRN_KDIM, HGRN_KDIM, HGRN_WIDTH, HGRN_WIDTH)
IN_WIDTH = 3 * SB_WIDTH + 3 * DIL_WIDTH + 2 * HGRN_KDIM + 2 * HGRN_WIDTH
D_FF = 2816
QBLK = 128
HGRN_CHUNK = 64
DIL_PATTERNS = ((128, 1), (512, 4), (2048, 16))
ROPE_THETA = 10000.0
EPS = 1e-6
LB_FLOOR = 1e-30
NEG_BIG = -1e30
N_MOD = 9
HALF_STEP = 0.5

kernel_name = "hymba_style_sb_dilated_hgrn2_macaron_block"


def _rmsnorm(x):
    xf = x.astype(jnp.float32)
    return (xf * lax.rsqrt(jnp.mean(xf * xf, axis=-1, keepdims=True) + EPS)).astype(x.dtype)


def _split_heads(x, n_heads):
    b, s, w = x.shape
    return x.reshape(b, s, n_heads, w // n_heads).transpose(0, 2, 1, 3)


def _merge_heads(x):
    b, h, s, d = x.shape
    return x.transpose(0, 2, 1, 3).reshape(b, s, h * d)


def _rope(x):
    s, d = x.shape[2], x.shape[3]
    half = d // 2
    inv_freq = ROPE_THETA ** (-jnp.arange(half, dtype=jnp.float32) * 2.0 / d)
    ang = jnp.arange(s, dtype=jnp.float32)[:, None] * inv_freq[None, :]
    cos, sin = jnp.cos(ang), jnp.sin(ang)
    xf = x.astype(jnp.float32)
    x1, x2 = xf[..., :half], xf[..., half:]
    return jnp.concatenate([x1 * cos - x2 * sin, x2 * cos + x1 * sin], axis=-1).astype(x.dtype)


def _swiglu(h, w_gate, w_up, w_down):
    return (jax.nn.silu(h @ w_gate) * (h @ w_up)) @ w_down


def _stick_breaking(q, k, v):
    b, h, s, d = q.shape
    nb = s // QBLK
    qf = q.astype(jnp.float32) * (d ** -0.5)
    kf = k.astype(jnp.float32)
    vf = v.astype(jnp.float32)
    q_blocks = qf.reshape(b, h, nb, QBLK, d).transpose(2, 0, 1, 3, 4)
    starts = jnp.arange(nb, dtype=jnp.int32) * QBLK
    kpos = jnp.arange(s, dtype=jnp.int32)

    def block(args):
        qb, start = args
        z = jnp.einsum('bhqd,bhkd->bhqk', qb, kf)
        qpos = start + jnp.arange(QBLK, dtype=jnp.int32)
        causal = kpos[None, :] < qpos[:, None]
        log_not_beta = jnp.where(causal, jax.nn.log_sigmoid(-z), 0.0)
        tail = lax.cumsum(log_not_beta, axis=3, reverse=True) - log_not_beta
        log_w = jnp.where(causal, jax.nn.log_sigmoid(z) + tail, NEG_BIG)
        w = jnp.exp(log_w)
        return jnp.einsum('bhqk,bhkd->bhqd', w, vf)

    out = lax.map(block, (q_blocks, starts))
    return out.transpose(1, 2, 0, 3, 4).reshape(b, h, s, d)


def _dilated_partial(q, k, v, window, dilation):
    b, h, s, d = q.shape
    steps = window // dilation
    sub_len = -(-s // (QBLK * dilation)) * QBLK
    pad = sub_len * dilation - s
    nb = sub_len // QBLK

    def to_residue(x):
        x = jnp.pad(x.astype(jnp.float32), ((0, 0), (0, 0), (0, pad), (0, 0)))
        x = x.reshape(b, h, sub_len, dilation, d).transpose(0, 1, 3, 2, 4)
        return x.reshape(b, h, dilation, nb, QBLK, d)

    qr, kr, vr = to_residue(q), to_residue(k), to_residue(v)

    def with_prev_block(x):
        prev = jnp.pad(x, ((0, 0), (0, 0), (0, 0), (1, 0), (0, 0), (0, 0)))[:, :, :, :-1]
        return jnp.concatenate([prev, x], axis=4)

    kw, vw = with_prev_block(kr), with_prev_block(vr)
    scores = jnp.einsum('bhrnqd,bhrnkd->bhrnqk', qr, kw)
    a_idx = jnp.arange(QBLK)[:, None]
    k_idx = jnp.arange(2 * QBLK)[None, :]
    dist = a_idx + QBLK - k_idx
    band = (dist >= 0) & (dist <= steps)
    valid = band[None] & ((jnp.arange(nb)[:, None, None] > 0) | (k_idx[None] >= QBLK))
    scores = jnp.where(valid, scores, NEG_BIG)
    mx = jnp.max(scores, axis=-1)
    p = jnp.where(valid, jnp.exp(scores - mx[..., None]), 0.0)
    den = jnp.sum(p, axis=-1)
    num = jnp.einsum('bhrnqk,bhrnkd->bhrnqd', p, vw)

    def back(x):
        x = x.reshape((b, h, dilation, sub_len) + x.shape[5:])
        perm = (0, 1, 3, 2) + tuple(range(4, x.ndim))
        x = x.transpose(perm).reshape((b, h, sub_len * dilation) + x.shape[4:])
        return x[:, :, :s]

    return back(num), back(den), back(mx)


def _dilated_mixture(q, k, v):
    parts = [_dilated_partial(q, k, v, w, r) for (w, r) in DIL_PATTERNS]
    m_all = parts[0][2]
    for _, _, m in parts[1:]:
        m_all = jnp.maximum(m_all, m)
    num = 0.0
    den = 0.0
    for n_i, d_i, m_i in parts:
        scale = jnp.exp(m_i - m_all)
        num = num + n_i * scale[..., None]
        den = den + d_i * scale
    return num / den[..., None]


def _hgrn2_chunkwise(q, k, v, log_f):
    b, h, s, dk = q.shape
    dv = v.shape[-1]
    n = s // HGRN_CHUNK

    def chunks(x):
        return x.astype(jnp.float32).reshape(b, h, n, HGRN_CHUNK, x.shape[-1]).transpose(2, 0, 1, 3, 4)

    qc, kc, vc = chunks(q), chunks(k), chunks(v)
    g_cum = jnp.cumsum(chunks(log_f), axis=3)
    causal = jnp.tril(jnp.ones((HGRN_CHUNK, HGRN_CHUNK), dtype=bool))[:, :, None]

    def step(state, xs):
        qb, kb, vb, gb = xs
        diff = gb[:, :, :, None, :] - gb[:, :, None, :, :]
        decay = jnp.where(causal, jnp.exp(jnp.where(causal, diff, 0.0)), 0.0)
        scores = jnp.einsum('bhtd,bhsd,bhtsd->bhts', qb, kb, decay)
        o = jnp.einsum('bhts,bhsv->bhtv', scores, vb) + jnp.einsum('bhtd,bhdv->bhtv', qb * jnp.exp(gb), state)
        g_last = gb[:, :, -1, :]
        new_state = jnp.exp(g_last)[..., None] * state + jnp.einsum('bhsd,bhsv->bhdv', kb * jnp.exp(g_last[:, :, None, :] - gb), vb)
        return new_state, o

    state0 = jnp.zeros((b, h, dk, dv), jnp.float32)
    _, o = lax.scan(step, state0, (qc, kc, vc, g_cum))
    return o.transpose(1, 2, 0, 3, 4).reshape(b, h, s, dv)


def _hybrid_mixer(h, w_in, w_out, q_norm_g, k_norm_g, hgrn_norm_g, lower_bound):
    dt = h.dtype
    proj = h @ w_in
    offsets = []
    acc = 0
    for wdt in IN_SPLITS[:-1]:
        acc += wdt
        offsets.append(acc)
    qa, ka, va, qd, kd, vd, qh, fh, ih, gh = jnp.split(proj, offsets, axis=-1)

    o_a = _stick_breaking(_split_heads(qa, SB_HEADS), _split_heads(ka, SB_HEADS), _split_heads(va, SB_HEADS))

    qd = _rope(_rmsnorm(_split_heads(qd, DIL_HEADS)) * q_norm_g) * (HEAD_DIM ** -0.5)
    kd = _rope(_rmsnorm(_split_heads(kd, DIL_HEADS)) * k_norm_g)
    o_d = _dilated_mixture(qd, kd, _split_heads(vd, DIL_HEADS))

    lb = lower_bound.reshape(HGRN_HEADS, 1, HGRN_DK).astype(jnp.float32)
    z = _split_heads(fh, HGRN_HEADS).astype(jnp.float32)
    log_f = jnp.logaddexp(jnp.log(jnp.maximum(lb, LB_FLOOR)), jnp.log1p(-lb) + jax.nn.log_sigmoid(z))
    k_h = -jnp.expm1(log_f)
    q_h = jax.nn.silu(_split_heads(qh, HGRN_HEADS))
    o_h = _hgrn2_chunkwise(q_h, k_h, _split_heads(ih, HGRN_HEADS), log_f)
    o_h = _rmsnorm(o_h) * hgrn_norm_g * jax.nn.silu(_split_heads(gh, HGRN_HEADS).astype(jnp.float32))

    y = jnp.concatenate([_merge_heads(o_a).astype(dt), _merge_heads(o_d).astype(dt), _merge_heads(o_h).astype(dt)], axis=-1)
    return y @ w_out


def setup_inputs(seed: int = 0) -> dict:
    key = jax.random.key(seed)
    ks = jax.random.split(key, 16)

    def nrm(k, shape, scale):
        return jax.random.normal(k, shape, jnp.float32) * scale

    return {
        "x": nrm(ks[0], (BATCH, SEQ, D_MODEL), 1.0),
        "c": nrm(ks[1], (BATCH, D_MODEL), 1.0),
        "w_mod": nrm(ks[2], (DEPTH, D_MODEL, N_MOD * D_MODEL), D_MODEL ** -0.5),
        "b_mod": nrm(ks[3], (DEPTH, N_MOD * D_MODEL), 0.02),
        "ffn1_w_gate": nrm(ks[4], (DEPTH, D_MODEL, D_FF), D_MODEL ** -0.5),
        "ffn1_w_up": nrm(ks[5], (DEPTH, D_MODEL, D_FF), D_MODEL ** -0.5),
        "ffn1_w_down": nrm(ks[6], (DEPTH, D_FF, D_MODEL), D_FF ** -0.5),
        "w_in": nrm(ks[7], (DEPTH, D_MODEL, IN_WIDTH), D_MODEL ** -0.5),
        "w_out": nrm(ks[8], (DEPTH, MIX_WIDTH, D_MODEL), MIX_WIDTH ** -0.5),
        "q_norm_g": 1.0 + nrm(ks[9], (DEPTH, HEAD_DIM), 0.02),
        "k_norm_g": 1.0 + nrm(ks[10], (DEPTH, HEAD_DIM), 0.02),
        "hgrn_norm_g": 1.0 + nrm(ks[11], (DEPTH, HGRN_DV), 0.02),
        "hgrn_lb_logits": nrm(ks[12], (DEPTH, HGRN_KDIM), 0.5),
        "ffn2_w_gate": nrm(ks[13], (DEPTH, D_MODEL, D_FF), D_MODEL ** -0.5),
        "ffn2_w_up": nrm(ks[14], (DEPTH, D_MODEL, D_FF), D_MODEL ** -0.5),
        "ffn2_w_down": nrm(ks[15], (DEPTH, D_FF, D_MODEL), D_FF ** -0.5),
    }


def reference(x, c, w_mod, b_mod, ffn1_w_gate, ffn1_w_up, ffn1_w_down, w_in, w_out, q_norm_g, k_norm_g, hgrn_norm_g, hgrn_lb_logits, ffn2_w_gate, ffn2_w_up, ffn2_w_down):
    lb_sm = jax.nn.softmax(hgrn_lb_logits.astype(jnp.float32), axis=0)
    lower_bounds = jnp.clip(jnp.cumsum(lb_sm, axis=0) - lb_sm[0:1], 0.0, 1.0 - EPS)
    for l in range(DEPTH):
        mod = jax.nn.silu(c) @ w_mod[l] + b_mod[l]
        sh1, sc1, g1, sh2, sc2, g2, sh3, sc3, g3 = jnp.split(mod[:, None, :], N_MOD, axis=-1)
        h = _rmsnorm(x) * (1.0 + sc1) + sh1
        x = x + HALF_STEP * g1 * _swiglu(h, ffn1_w_gate[l], ffn1_w_up[l], ffn1_w_down[l])
        h = _rmsnorm(x) * (1.0 + sc2) + sh2
        x = x + g2 * _hybrid_mixer(h, w_in[l], w_out[l], q_norm_g[l], k_norm_g[l], hgrn_norm_g[l], lower_bounds[l])
        h = _rmsnorm(x) * (1.0 + sc3) + sh3
        x = x + HALF_STEP * g3 * _swiglu(h, ffn2_w_gate[l], ffn2_w_up[l], ffn2_w_down[l])
    return x
```

```python
import numpy as np
import ml_dtypes
import concourse.bass as bass
import concourse.mybir as mybir
from concourse.bass_utils import run_bass_kernel_spmd


F32 = mybir.dt.float32
BF16 = mybir.dt.bfloat16
AF = mybir.ActivationFunctionType
ALU = mybir.AluOpType
AX = mybir.AxisListType


class Buf:
    __slots__ = ("name", "last_w", "reads", "dsem")

    def __init__(self, name):
        self.name = name
        self.last_w = None
        self.reads = []
        self.dsem = None


class DSem:
    def __init__(self, prog, name):
        self.h = prog.nc.alloc_semaphore(name)
        self.cnt = 0


class Prog:
    COMPUTE = ("pe", "act", "dve", "pool")
    QUEUES = ("pe", "act", "dve", "pool", "sp")

    def __init__(self, nc):
        self.nc = nc
        self.q = {e: [] for e in self.QUEUES}
        self.sem = {e: nc.alloc_semaphore("s_" + e) for e in self.COMPUTE}
        self.cnt = {e: 0 for e in self.COMPUTE}
        self.known = {e: {} for e in self.QUEUES}
        self.semh = {}
        for e in self.COMPUTE:
            self.semh[("e", e)] = self.sem[e]
        self.dsems = []
        self.nbuf = 0
        self.ninstr = {e: 0 for e in self.QUEUES}

    def buf(self, name=None):
        self.nbuf += 1
        return Buf(name or f"b{self.nbuf}")

    def dsem(self, name):
        d = DSem(self, name)
        key = ("d", len(self.dsems))
        d.key = key
        self.semh[key] = d.h
        self.dsems.append(d)
        return d

    def sb(self, name, shape, dtype):
        return self.nc.alloc_sbuf_tensor(name, list(shape), dtype)

    def ps(self, name, shape, dtype=F32):
        return self.nc.alloc_psum_tensor(name, list(shape), dtype)

    def _deps(self, reads, writes):
        deps = {}
        def add(tok):
            if tok is None:
                return
            k, v = tok
            if deps.get(k, 0) < v:
                deps[k] = v
        for b in reads:
            add(b.last_w)
        for b in writes:
            add(b.last_w)
            for t in b.reads:
                add(t)
        return deps

    def _waits(self, eng, deps):
        out = []
        kn = self.known[eng]
        for k, v in deps.items():
            if k[0] == "d":
                d = self.dsems[k[1]]
                v = d.cnt
            if eng == "pe" and k == ("e", "pe"):
                continue
            if kn.get(k, 0) < v:
                kn[k] = v
                out.append((self.semh[k], v))
        return out

    def _commit(self, tok, reads, writes):
        for b in reads:
            b.reads.append(tok)
        for b in writes:
            b.last_w = tok
            b.reads = []

    def op(self, eng, method, reads=(), writes=(), **kw):
        waits = self._waits(eng, self._deps(reads, writes))
        self.cnt[eng] += 1
        sem = self.sem[eng]
        def run(e, method=method, kw=kw, waits=waits, sem=sem):
            for h, v in waits:
                e.wait_ge(h, v)
            getattr(e, method)(**kw).then_inc(sem, 1)
        self.q[eng].append(run)
        self.ninstr[eng] += 1
        tok = (("e", eng), self.cnt[eng])
        self._commit(tok, reads, writes)
        return tok

    def mm(self, mms, reads=(), writes=()):
        waits = self._waits("pe", self._deps(reads, writes))
        self.cnt["pe"] += 1
        sem = self.sem["pe"]
        def run(e, mms=mms, waits=waits, sem=sem):
            for h, v in waits:
                e.wait_ge(h, v)
            ins = None
            for kw in mms:
                ins = e.matmul(**kw)
            ins.then_inc(sem, 1)
        self.q["pe"].append(run)
        self.ninstr["pe"] += len(mms)
        tok = (("e", "pe"), self.cnt["pe"])
        self._commit(tok, reads, writes)
        return tok

    def dma(self, queue, out, in_, dsem, reads=(), writes=(), **kw):
        waits = self._waits(queue, self._deps(reads, writes))
        dsem.cnt += 16
        h = dsem.h
        def run(e, out=out, in_=in_, waits=waits, h=h, kw=kw):
            for hh, v in waits:
                e.wait_ge(hh, v)
            e.dma_start(out=out, in_=in_, **kw).then_inc(h, 16)
        self.q[queue].append(run)
        self.ninstr[queue] += 1
        tok = (dsem.key, dsem.cnt)
        self._commit(tok, reads, writes)
        return tok

    def finish(self):
        waits = [(d.h, d.cnt) for d in self.dsems if d.cnt > 0]
        def run(e, waits=waits):
            for h, v in waits:
                e.wait_ge(h, v)
        self.q["sp"].append(run)

    def build(self):
        self.finish()
        q = self.q
        with self.nc.Block() as block:
            @block.tensor
            def _(e):
                for f in q["pe"]:
                    f(e)

            @block.scalar
            def _(e):
                for f in q["act"]:
                    f(e)

            @block.vector
            def _(e):
                for f in q["dve"]:
                    f(e)

            @block.gpsimd
            def _(e):
                for f in q["pool"]:
                    f(e)

            @block.sync
            def _(e):
                for f in q["sp"]:
                    f(e)


D = 1024
DFF = 2816
NF = DFF // 128
NTOK = 4096
T = 512
NT = NTOK // T
INW = 3584
EPS = 1e-6
U8 = mybir.dt.uint8
ARENA_BYTES = 206 * 1024
DEBUG_MOD = False


class Arena:
    def __init__(self, nc):
        self.t = nc.alloc_sbuf_tensor("arena", [128, ARENA_BYTES], U8)
        self.off = 0
        self.base = 0

    def alloc(self, shape, dtype, parts=128):
        esz = mybir.dt.size(dtype)
        n = 1
        for s in shape:
            n *= s
        nb = n * esz
        off = (self.off + 31) // 32 * 32
        assert off + nb <= ARENA_BYTES, ("arena overflow", off + nb)
        self.off = off + nb
        v = self.t[0:parts, off:off + nb].bitcast(dtype)
        if len(shape) == 2:
            v = v.rearrange("p (a b) -> p a b", a=shape[0])
        elif len(shape) == 3:
            v = v.rearrange("p (a b c) -> p a b c", a=shape[0], b=shape[1])
        return v

    def mark(self):
        self.base = self.off

    def reset(self):
        self.off = self.base


def barrier(P):
    toks = [(("e", e), P.cnt[e]) for e in P.COMPUTE if P.cnt[e] > 0]
    toks += [(d.key, d.cnt) for d in P.dsems if d.cnt > 0]
    for q in P.QUEUES:
        waits = []
        kn = P.known[q]
        for k, v in toks:
            if q == k[1] and k[0] == "e":
                continue
            if kn.get(k, 0) < v:
                kn[k] = v
                waits.append((P.semh[k], v))
        def run(e, waits=waits):
            for h, v in waits:
                e.wait_ge(h, v)
        P.q[q].append(run)


def build_tok(stages, layers_mod):
    nc = bass.Bass("TRN2", target_bir_lowering=False)
    P = Prog(nc)
    A = Arena(nc)

    def din(name, shape, dt=F32):
        return nc.dram_tensor(name, list(shape), dt, kind="ExternalInput").ap()

    def dout(name, shape, dt=F32):
        return nc.dram_tensor(name, list(shape), dt, kind="ExternalOutput").ap()

    xT = din("xT", [D, NTOK])
    cT = din("cT", [128, 8])
    xo = dout("xo", [D, NTOK])
    xT_v = xT.rearrange("(c p) t -> p c t", p=128)
    xo_v = xo.rearrange("(c p) t -> p c t", p=128)
    XD = [P.buf(f"xd{i}") for i in range(NT)]
    has_proj = any(s[0] == "proj" for s in stages)
    has_wout = any(s[0] == "wout" for s in stages)
    if has_proj:
        pfb = dout("pfb", [512, NTOK], BF16)
        pff = dout("pff", [2048, NTOK], F32)
        ptb = dout("ptb", [NTOK, 1024], BF16)
    if has_wout:
        yT = din("yT", [D, NTOK], BF16)
        yT_v = yT.rearrange("(c p) t -> p c t", p=128)

    banks = [P.ps(f"bank{i}", [128, 512]) for i in range(8)]
    BK = [P.buf(f"bk{i}") for i in range(8)]

    ld = P.dsem("ld")
    wl = P.dsem("wl")
    st = P.dsem("st")
    ml = P.dsem("ml")

    ones_bf = A.alloc([128], BF16); ONES = P.buf("ones")
    P.op("dve", "memset", ap=ones_bf, constant=1.0, writes=[ONES])
    csb = A.alloc([8], F32); CSB = P.buf("c")
    P.dma("sp", csb, cT, ml, writes=[CSB])
    P.op("act", "activation", out=csb, in_=csb, func=AF.Silu, reads=[CSB], writes=[CSB])
    modv = {}
    MODB = {}
    bmods = {}
    for l, groups in layers_mod.items():
        bm = din(f"bmod{l}", [128, 72])
        bsb = A.alloc([72], F32); BSB = P.buf()
        P.dma("sp", bsb, bm, ml, writes=[BSB])
        bmods[l] = (bsb, BSB)
        for g in groups:
            modv[(l, g)] = A.alloc([8], F32)
            MODB[(l, g)] = P.buf(f"mod{l}_{g}")
    A.mark()

    wmp = [A.alloc([8, 512], F32) for _ in range(2)]
    WMP = [P.buf("wmp0"), P.buf("wmp1")]
    k = 0
    for l, groups in layers_mod.items():
        wm = din(f"wmod{l}", [D, 9 * D])
        wm_v = wm.rearrange("(c p) n -> p c n", p=128)
        bsb, BSB = bmods[l]
        for g in groups:
            bank = banks[k % 2]; BANK = BK[k % 2]
            for hf in range(2):
                slot = k % 2
                col0 = g * 1024 + hf * 512
                P.dma("sp", wmp[slot], wm_v[:, :, col0:col0 + 512], ml, writes=[WMP[slot]])
                mms = []
                for j in range(4):
                    for c in range(8):
                        mms.append(dict(out=bank[:, hf * 4 + j: hf * 4 + j + 1],
                                        lhsT=wmp[slot][:, c, j * 128:(j + 1) * 128],
                                        rhs=csb[:, c:c + 1], start=(c == 0), stop=(c == 7)))
                P.mm(mms, reads=[WMP[slot], CSB], writes=[BANK])
                k += 1
            mv = modv[(l, g)]
            P.op("dve", "tensor_tensor",
                out=mv, in0=bank[:, 0:8], in1=bsb[:, g * 8:(g + 1) * 8], op=ALU.add,
                reads=[BANK, BSB], writes=[MODB[(l, g)]])
            if g in (1, 4, 7):
                P.op("dve", "tensor_scalar_add", out=mv, in0=mv, scalar1=1.0,
                     reads=[MODB[(l, g)]], writes=[MODB[(l, g)]])
            if g in (2, 8):
                P.op("dve", "tensor_scalar_mul", out=mv, in0=mv, scalar1=0.5,
                     reads=[MODB[(l, g)]], writes=[MODB[(l, g)]])

    if DEBUG_MOD:
        keys = sorted(modv.keys())
        dbg = dout("dbg", [128, 8 * len(keys)])
        for n, kk_ in enumerate(keys):
            P.dma("sp", dbg[:, n * 8:(n + 1) * 8], modv[kk_], st, reads=[MODB[kk_]])
        dbgc = dout("dbgc", [128, 8])
        P.dma("sp", dbgc, csb, st, reads=[CSB])
    first_x = [True]

    def x_src(i):
        v = xT_v if first_x[0] else xo_v
        return v[:, :, i * T:(i + 1) * T]

    def norm_mod(xt, XT, sqv, SQ, hbuf, HB, rstd, RS, tmps, TMPS, l, gsh, gsc):
        NB = 7
        P.op("act", "activation", out=sqv, in_=xt, func=AF.Square, reads=[XT], writes=[SQ])
        P.mm([dict(out=banks[NB][:], lhsT=ones_bf, rhs=sqv[:, c, :], start=(c == 0), stop=(c == 7))
              for c in range(8)], reads=[SQ, ONES], writes=[BK[NB]])
        P.op("act", "activation", out=rstd, in_=banks[NB][:], func=AF.Sqrt, bias=EPS, scale=1.0 / D,
             reads=[BK[NB]], writes=[RS])
        P.op("dve", "reciprocal", out=rstd, in_=rstd, reads=[RS], writes=[RS])
        ops = modv[(l, gsc)]; sh = modv[(l, gsh)]
        for c in range(8):
            tmp = tmps[c % 2]; TMP = TMPS[c % 2]
            P.op("dve", "scalar_tensor_tensor",
                out=tmp, in0=xt[:, c, :], scalar=ops[:, c:c + 1], in1=rstd, op0=ALU.mult, op1=ALU.mult,
                reads=[XT, RS, MODB[(l, gsc)]], writes=[TMP])
            P.op("act", "activation",
                out=hbuf[:, c, :], in_=tmp, func=AF.Identity, bias=sh[:, c:c + 1], scale=1.0,
                reads=[TMP, MODB[(l, gsh)]], writes=[HB])

    for stage in stages:
        barrier(P)
        A.reset()
        kind = stage[0]
        l = stage[1]
        if kind == "ffn":
            which = stage[2]
            g0 = 0 if which == 1 else 6
            wg_d = din(f"wg{l}_{which}", [D, DFF]).rearrange("(c p) f -> p c f", p=128)
            wu_d = din(f"wu{l}_{which}", [D, DFF]).rearrange("(c p) f -> p c f", p=128)
            wd_d = din(f"wd{l}_{which}", [DFF, D]).rearrange("(f p) d -> p f d", p=128)
            wg = A.alloc([8, DFF], BF16); WG = P.buf("wg")
            wu = A.alloc([8, DFF], BF16); WU = P.buf("wu")
            wd = A.alloc([NF, D], BF16); WD = P.buf("wd")
            for c in range(8):
                P.dma("pool", wg[:, c, :], wg_d[:, c, :], wl, writes=[WG])
                P.dma("pool", wu[:, c, :], wu_d[:, c, :], wl, writes=[WU])
            for f in range(0, NF, 2):
                P.dma("pool", wd[:, f:f + 2, :], wd_d[:, f:f + 2, :], wl, writes=[WD])
            xt = A.alloc([8, T], F32); XT = P.buf("xt")
            hb = A.alloc([8, T], BF16); HB = P.buf("h")
            ab = A.alloc([NF, T], BF16); AB = P.buf("a")
            rstd = A.alloc([T], F32); RS = P.buf("rstd")
            tmps = [A.alloc([T], F32) for _ in range(2)]; TMPS = [P.buf(), P.buf()]
            sgs = [A.alloc([T], F32) for _ in range(2)]; SGS = [P.buf(), P.buf()]
            sqv = ab[:, 0:8, :]
            hg = modv[(l, g0 + 2)]
            for i in range(NT):
                P.dma("sp", xt, x_src(i), ld, reads=[XD[i]], writes=[XT])
                norm_mod(xt, XT, sqv, AB, hb, HB, rstd, RS, tmps, TMPS, l, g0, g0 + 1)
                for f in range(NF):
                    bg = 0 + f % 2; bu = 2 + f % 2
                    P.mm([dict(out=banks[bg][:], lhsT=wg[:, c, f * 128:(f + 1) * 128], rhs=hb[:, c, :],
                               start=(c == 0), stop=(c == 7)) for c in range(8)],
                         reads=[WG, HB], writes=[BK[bg]])
                    P.mm([dict(out=banks[bu][:], lhsT=wu[:, c, f * 128:(f + 1) * 128], rhs=hb[:, c, :],
                               start=(c == 0), stop=(c == 7)) for c in range(8)],
                         reads=[WU, HB], writes=[BK[bu]])
                    sg = sgs[f % 2]; SG = SGS[f % 2]
                    P.op("act", "activation", out=sg, in_=banks[bg][:], func=AF.Silu,
                         reads=[BK[bg]], writes=[SG])
                    P.op("dve", "tensor_tensor",
                        out=ab[:, f, :], in0=sg, in1=banks[bu][:], op=ALU.mult,
                        reads=[SG, BK[bu]], writes=[AB])
                for dc in range(8):
                    bo = 4 + dc % 2
                    P.mm([dict(out=banks[bo][:], lhsT=wd[:, f, dc * 128:(dc + 1) * 128], rhs=ab[:, f, :],
                               start=(f == 0), stop=(f == NF - 1)) for f in range(NF)],
                         reads=[WD, AB], writes=[BK[bo]])
                    P.op("dve", "scalar_tensor_tensor",
                        out=xt[:, dc, :], in0=banks[bo][:], scalar=hg[:, dc:dc + 1], in1=xt[:, dc, :],
                        op0=ALU.mult, op1=ALU.add,
                        reads=[BK[bo], XT, MODB[(l, g0 + 2)]], writes=[XT])
                P.dma("sp", xo_v[:, :, i * T:(i + 1) * T], xt, st, reads=[XT], writes=[XD[i]])
            first_x[0] = False
            if DEBUG_MOD:
                d_h = dout("d_h", [128, 8, T], BF16)
                d_a = dout("d_a", [128, NF, T], BF16)
                d_wd = dout("d_wd", [128, NF, D], BF16)
                d_wg = dout("d_wg", [128, 8, DFF], BF16)
                d_rs = dout("d_rs", [128, T], F32)
                P.dma("sp", d_h, hb, st, reads=[HB])
                P.dma("sp", d_a, ab, st, reads=[AB])
                P.dma("sp", d_wd, wd, st, reads=[WD])
                P.dma("sp", d_wg, wg, st, reads=[WG])
                P.dma("sp", d_rs, rstd, st, reads=[RS])

        elif kind == "wout":
            wo_d = din(f"wout{l}", [D, D]).rearrange("(c p) d -> p c d", p=128)
            wo = A.alloc([8, D], BF16); WO = P.buf("wo")
            P.dma("pool", wo, wo_d, wl, writes=[WO])
            xts = [A.alloc([8, T], F32) for _ in range(2)]; XTS = [P.buf(), P.buf()]
            yts = [A.alloc([8, T], BF16) for _ in range(2)]; YTS = [P.buf(), P.buf()]
            g2 = modv[(l, 5)]
            for i in range(NT):
                xt = xts[i % 2]; XT = XTS[i % 2]; yt = yts[i % 2]; YT = YTS[i % 2]
                P.dma("sp", xt, x_src(i), ld, reads=[XD[i]], writes=[XT])
                P.dma("sp", yt, yT_v[:, :, i * T:(i + 1) * T], ld, writes=[YT])
                for dc in range(8):
                    bo = dc % 4
                    P.mm([dict(out=banks[bo][:], lhsT=wo[:, c, dc * 128:(dc + 1) * 128], rhs=yt[:, c, :],
                               start=(c == 0), stop=(c == 7)) for c in range(8)],
                         reads=[WO, YT], writes=[BK[bo]])
                    P.op("dve", "scalar_tensor_tensor",
                        out=xt[:, dc, :], in0=banks[bo][:], scalar=g2[:, dc:dc + 1], in1=xt[:, dc, :],
                        op0=ALU.mult, op1=ALU.add,
                        reads=[BK[bo], XT, MODB[(l, 5)]], writes=[XT])
                P.dma("sp", xo_v[:, :, i * T:(i + 1) * T], xt, st, reads=[XT], writes=[XD[i]])
            first_x[0] = False

        elif kind == "proj":
            wi_d = din(f"win{l}", [D, INW]).rearrange("(c p) f -> p c f", p=128)
            wi = A.alloc([8, INW], BF16); WI = P.buf("wi")
            for c in range(8):
                P.dma("pool", wi[:, c, :], wi_d[:, c, :], wl, writes=[WI])
            xts = [A.alloc([8, T], F32) for _ in range(2)]; XTS = [P.buf(), P.buf()]
            hb = A.alloc([8, T], BF16); HB = P.buf("h")
            sqv = A.alloc([8, T], BF16); SQ = P.buf("sq")
            rstd = A.alloc([T], F32); RS = P.buf("rstd")
            tmps = [A.alloc([T], F32) for _ in range(2)]; TMPS = [P.buf(), P.buf()]
            NST = 4
            stf = [A.alloc([T], F32) for _ in range(NST)]; STF = [P.buf() for _ in range(NST)]
            stb = [A.alloc([T], BF16) for _ in range(NST)]; STB = [P.buf() for _ in range(NST)]
            fm = []
            for j in range(4):
                fm.append((j * 128, "b", j * 128, 0.125 if j < 2 else 1.0))
            for j in range(4):
                fm.append((768 + j * 128, "f", j * 128, 1.0))
            for j in range(8):
                fm.append((1536 + j * 128, "f", 512 + j * 128, 1.0))
            for j in range(4):
                fm.append((3072 + j * 128, "f", 1536 + j * 128, 1.0))
            kk = 0
            for i in range(NT):
                xt = xts[i % 2]; XT = XTS[i % 2]
                P.dma("sp", xt, x_src(i), ld, reads=[XD[i]], writes=[XT])
                norm_mod(xt, XT, sqv, SQ, hb, HB, rstd, RS, tmps, TMPS, l, 3, 4)
                for (col0, dk, row0, scl) in fm:
                    bo = kk % 6; s = kk % NST
                    P.mm([dict(out=banks[bo][:], lhsT=wi[:, c, col0:col0 + 128], rhs=hb[:, c, :],
                               start=(c == 0), stop=(c == 7)) for c in range(8)],
                         reads=[WI, HB], writes=[BK[bo]])
                    if dk == "b":
                        dst = stb[s]; DST = STB[s]
                        dd = pfb[row0:row0 + 128, i * T:(i + 1) * T]
                    else:
                        dst = stf[s]; DST = STF[s]
                        dd = pff[row0:row0 + 128, i * T:(i + 1) * T]
                    if kk % 2 == 0:
                        P.op("act", "activation",
                            out=dst, in_=banks[bo][:], func=AF.Copy, scale=scl, reads=[BK[bo]], writes=[DST])
                    else:
                        P.op("dve", "tensor_scalar_mul",
                            out=dst, in0=banks[bo][:], scalar1=scl, reads=[BK[bo]], writes=[DST])
                    P.dma("sp", dd, dst, st, reads=[DST])
                    kk += 1
                for sblk in range(4):
                    for grp in range(2):
                        bo = kk % 6; s = kk % NST
                        tsl = slice(sblk * 128, (sblk + 1) * 128)
                        if grp == 0:
                            mms = [dict(out=banks[bo][:, 0:256], lhsT=hb[:, c, tsl], rhs=wi[:, c, 512:768],
                                        start=(c == 0), stop=(c == 7)) for c in range(8)]
                            mms += [dict(out=banks[bo][:, 256:512], lhsT=hb[:, c, tsl], rhs=wi[:, c, 1280:1536],
                                         start=(c == 0), stop=(c == 7)) for c in range(8)]
                        else:
                            mms = [dict(out=banks[bo][:], lhsT=hb[:, c, tsl], rhs=wi[:, c, 2560:3072],
                                        start=(c == 0), stop=(c == 7)) for c in range(8)]
                        P.mm(mms, reads=[WI, HB], writes=[BK[bo]])
                        dst = stb[s]; DST = STB[s]
                        if kk % 2 == 0:
                            P.op("act", "copy", out=dst, in_=banks[bo][:],
                                 reads=[BK[bo]], writes=[DST])
                        else:
                            P.op("dve", "tensor_copy", out=dst, in_=banks[bo][:],
                                 reads=[BK[bo]], writes=[DST])
                        r0 = i * T + sblk * 128
                        P.dma("sp", ptb[r0:r0 + 128, grp * 512:(grp + 1) * 512], dst, st, reads=[DST])
                        kk += 1
    P.build()
    return nc, P


S = 16384
HD = 64
EPS = 1e-6
QT = 512
NQT = S // QT
NKB = S // 128


def build_mix(layer, phases=("sb", "dil", "hgrn")):
    nc = bass.Bass("TRN2", target_bir_lowering=False)
    P = Prog(nc)
    A = Arena(nc)

    def din(name, shape, dt=F32):
        return nc.dram_tensor(name, list(shape), dt, kind="ExternalInput").ap()

    def dout(name, shape, dt=F32):
        return nc.dram_tensor(name, list(shape), dt, kind="ExternalOutput").ap()

    qk = din("qk", [2, HD, S], BF16)
    va = din("va", [S, HD], BF16)
    qkd = din("qkd", [2, HD, S], F32)
    vd = din("vd", [S, HD], BF16)
    hg = din("hg", [3, 128, S], F32)
    ih = din("ih", [S, 128], BF16)
    cosT = din("cosT", [HD, S], F32)
    sinT = din("sinT", [HD, S], F32)
    gn = din("gn", [HD, 4], F32)
    hgn = din("hgn", [128, 1], F32)
    lbl = din("lbl", [128, 2], F32)
    msb = din("msb", [4, 128, QT], BF16)
    lneg_d = din("lneg", [128, 128], BF16)
    mdil_d = din("mdil", [128, 256], BF16)
    mh_d = din("mh", [64, 64], BF16)
    ident_d = din("ident", [128, 128], BF16)
    rmask_d = din("rmask", [128, 512], F32)
    yT = dout("yT", [256, S], BF16)

    banks = [P.ps(f"bank{i}", [128, 512]) for i in range(8)]
    BK = [P.buf(f"bk{i}") for i in range(8)]

    cl = P.dsem("cl")
    ld = P.dsem("ld")
    ld2 = P.dsem("ld2")
    st = P.dsem("st")

    ones_bf = A.alloc([128], BF16); ONES = P.buf("ones")
    P.op("dve", "memset", ap=ones_bf, constant=1.0, writes=[ONES])
    lneg = A.alloc([128], BF16); LNEG = P.buf("lneg")
    P.dma("sp", lneg, lneg_d, cl, writes=[LNEG])
    A.mark()

    if "sb" in phases:
        q_sb = A.alloc([S], BF16, parts=64); QSB = P.buf("q_sb")
        k_sb = A.alloc([S], BF16, parts=64); KSB = P.buf("k_sb")
        v_sb = A.alloc([NKB, HD], BF16); VSB = P.buf("v_sb")
        mk = A.alloc([4, QT], BF16); MK = P.buf("mk")
        for h4 in range(4):
            sl = slice(h4 * (S // 4), (h4 + 1) * (S // 4))
            P.dma("sp", q_sb[:, sl], qk[0, :, sl], ld, writes=[QSB])
            P.dma("sp", k_sb[:, sl], qk[1, :, sl], ld, writes=[KSB])
        P.dma("sp", v_sb, va.rearrange("(n p) d -> p n d", p=128), ld, writes=[VSB])
        P.dma("sp", mk, msb.rearrange("r p q -> p r q"), cl, writes=[MK])
        NB3 = 3
        e_sb = [A.alloc([QT], F32) for _ in range(2)]; ESB = [P.buf() for _ in range(2)]
        sp_sb = [A.alloc([QT], BF16) for _ in range(NB3)]; SPSB = [P.buf() for _ in range(NB3)]
        tt_sb = [A.alloc([QT], F32) for _ in range(NB3)]; TTSB = [P.buf() for _ in range(NB3)]
        w_sb = [A.alloc([QT], BF16) for _ in range(NB3)]; WSB = [P.buf() for _ in range(NB3)]
        carry = A.alloc([QT], F32); CARRY = P.buf("carry")
        osb = [A.alloc([QT], BF16, parts=64) for _ in range(2)]; OSB = [P.buf() for _ in range(2)]
        cells = []
        for qt in range(NQT):
            kbs = list(range(4 * qt + 3, -1, -1))
            for idx, kb in enumerate(kbs):
                cells.append((qt, kb, idx == 0, idx == len(kbs) - 1))
        NC_ = len(cells)

        def s1(n):
            qt, kb, first, last = cells[n]
            P.mm([dict(out=banks[n % 2][:], lhsT=k_sb[:, kb * 128:(kb + 1) * 128], rhs=q_sb[:, qt * QT:(qt + 1) * QT],
                       start=True, stop=True)], reads=[KSB, QSB], writes=[BK[n % 2]])

        def s2(n):
            qt, kb, first, last = cells[n]
            r = kb - 4 * qt
            P.op("act", "activation", out=e_sb[n % 2], in_=banks[n % 2][:], func=AF.Exp,
                 reads=[BK[n % 2]], writes=[ESB[n % 2]])
            P.op("act", "activation", out=sp_sb[n % NB3], in_=e_sb[n % 2], func=AF.Ln, bias=1.0, scale=1.0,
                 reads=[ESB[n % 2]], writes=[SPSB[n % NB3]])
            if r >= 0:
                P.op("pool", "tensor_tensor", out=sp_sb[n % NB3], in0=sp_sb[n % NB3], in1=mk[:, r, :], op=ALU.mult,
                     reads=[SPSB[n % NB3], MK], writes=[SPSB[n % NB3]])

        def s3(n):
            qt, kb, first, last = cells[n]
            bp = 2 + n % 2; bc = 4 + n % 2
            P.mm([dict(out=banks[bp][:], lhsT=k_sb[:, kb * 128:(kb + 1) * 128], rhs=q_sb[:, qt * QT:(qt + 1) * QT],
                       start=True, stop=False),
                  dict(out=banks[bp][:], lhsT=lneg, rhs=sp_sb[n % NB3], start=False, stop=True)],
                 reads=[KSB, QSB, LNEG, SPSB[n % NB3]], writes=[BK[bp]])
            P.mm([dict(out=banks[bc][:], lhsT=ones_bf, rhs=sp_sb[n % NB3], start=True, stop=True)],
                 reads=[ONES, SPSB[n % NB3]], writes=[BK[bc]])

        def s4(n):
            qt, kb, first, last = cells[n]
            bp = 2 + n % 2; bc = 4 + n % 2
            if first:
                P.op("dve", "tensor_copy", out=tt_sb[n % NB3], in_=banks[bp][:], reads=[BK[bp]], writes=[TTSB[n % NB3]])
                P.op("dve", "tensor_copy", out=carry, in_=banks[bc][:], reads=[BK[bc]], writes=[CARRY])
            else:
                P.op("dve", "tensor_tensor", out=tt_sb[n % NB3], in0=banks[bp][:], in1=carry, op=ALU.subtract,
                     reads=[BK[bp], CARRY], writes=[TTSB[n % NB3]])
                P.op("dve", "tensor_tensor", out=carry, in0=banks[bc][:], in1=carry, op=ALU.add,
                     reads=[BK[bc], CARRY], writes=[CARRY])

        def s5(n):
            qt, kb, first, last = cells[n]
            r = kb - 4 * qt
            P.op("act", "activation", out=w_sb[n % NB3], in_=tt_sb[n % NB3], func=AF.Exp,
                 reads=[TTSB[n % NB3]], writes=[WSB[n % NB3]])
            if r >= 0:
                P.op("pool", "tensor_tensor", out=w_sb[n % NB3], in0=w_sb[n % NB3], in1=mk[:, r, :], op=ALU.mult,
                     reads=[WSB[n % NB3], MK], writes=[WSB[n % NB3]])

        def s6(n):
            qt, kb, first, last = cells[n]
            bo = 6 + qt % 2
            P.mm([dict(out=banks[bo][0:64, :], lhsT=v_sb[:, kb, :], rhs=w_sb[n % NB3], start=first, stop=last)],
                 reads=[VSB, WSB[n % NB3]], writes=[BK[bo]])
            if last:
                o = osb[qt % 2]; O = OSB[qt % 2]
                P.op("act", "copy", out=o, in_=banks[bo][0:64, :], reads=[BK[bo]], writes=[O])
                P.dma("sp", yT[0:64, qt * QT:(qt + 1) * QT], o, st, reads=[O])

        stages_ = [s1, s2, s3, s4, s5, s6]
        for step in range(NC_ + 5):
            for k_, fn in enumerate(stages_):
                n = step - k_
                if 0 <= n < NC_:
                    fn(n)

    if "dil" in phases:
        barrier(P)
        A.reset()
        qd_sb = A.alloc([S], BF16, parts=64); QD = P.buf("qd_sb")
        kd_sb = A.alloc([S], BF16, parts=64); KD = P.buf("kd_sb")
        gsb = A.alloc([4], F32, parts=64); GSB = P.buf("gsb")
        md = A.alloc([256], BF16); MD = P.buf("md")
        P.dma("sp", gsb, gn, cl, writes=[GSB])
        P.dma("sp", md, mdil_d, cl, writes=[MD])
        P.op("dve", "tensor_scalar_mul", out=gsb[:, 0:2], in0=gsb[:, 0:2], scalar1=HD ** -0.5, reads=[GSB], writes=[GSB])
        vds = {}
        VDS = {}
        for r in (1, 4, 16):
            nb = S // (128 * r)
            vds[r] = A.alloc([nb, r * HD], BF16); VDS[r] = P.buf(f"vd{r}")
            P.dma("sp", vds[r], vd.rearrange("(n a r) d -> a n (r d)", a=128, r=r), ld, writes=[VDS[r]])
        TP = 512
        NB2 = 2
        cs = [A.alloc([TP], F32, parts=64) for _ in range(NB2)]; CS = [P.buf() for _ in range(NB2)]
        sn = [A.alloc([TP], F32, parts=64) for _ in range(NB2)]; SN = [P.buf() for _ in range(NB2)]
        xr = [A.alloc([TP], F32, parts=64) for _ in range(NB2)]; XR = [P.buf() for _ in range(NB2)]
        xs = [A.alloc([TP], F32, parts=64) for _ in range(NB2)]; XS = [P.buf() for _ in range(NB2)]
        sq = [A.alloc([TP], BF16, parts=64) for _ in range(NB2)]; SQ = [P.buf() for _ in range(NB2)]
        rs = [A.alloc([TP], F32, parts=64) for _ in range(NB2)]; RSB = [P.buf() for _ in range(NB2)]
        ta = [A.alloc([TP], F32, parts=64) for _ in range(NB2)]; TA = [P.buf() for _ in range(NB2)]
        tb = [A.alloc([TP], F32, parts=64) for _ in range(NB2)]; TB = [P.buf() for _ in range(NB2)]
        kk = 0
        for t in range(S // TP):
            tsl = slice(t * TP, (t + 1) * TP)
            c_ = cs[t % NB2]; C_ = CS[t % NB2]; s_ = sn[t % NB2]; S_ = SN[t % NB2]
            P.dma("sp", c_, cosT[:, tsl], ld2, writes=[C_])
            P.dma("sp", s_, sinT[:, tsl], ld2, writes=[S_])
            for w in range(2):
                j = kk % NB2
                dst = qd_sb if w == 0 else kd_sb
                DST = QD if w == 0 else KD
                P.dma("sp", xr[j], qkd[w, :, tsl], ld2, writes=[XR[j]])
                P.dma("sp", xs[j][0:32, :], qkd[w, 32:64, tsl], ld2, writes=[XS[j]])
                P.dma("sp", xs[j][32:64, :], qkd[w, 0:32, tsl], ld2, writes=[XS[j]])
                P.op("act", "activation", out=sq[j], in_=xr[j], func=AF.Square, reads=[XR[j]], writes=[SQ[j]])
                bn = kk % 2
                P.mm([dict(out=banks[bn][0:64, :], lhsT=ones_bf[0:64, 0:64], rhs=sq[j], start=True, stop=True)],
                     reads=[ONES, SQ[j]], writes=[BK[bn]])
                P.op("act", "activation", out=rs[j], in_=banks[bn][0:64, :], func=AF.Sqrt, bias=EPS, scale=1.0 / HD,
                     reads=[BK[bn]], writes=[RSB[j]])
                P.op("dve", "reciprocal", out=rs[j], in_=rs[j], reads=[RSB[j]], writes=[RSB[j]])
                P.op("dve", "scalar_tensor_tensor", out=ta[j], in0=xr[j], scalar=gsb[:, 2 * w:2 * w + 1], in1=c_,
                     op0=ALU.mult, op1=ALU.mult, reads=[XR[j], GSB, C_], writes=[TA[j]])
                P.op("dve", "scalar_tensor_tensor", out=tb[j], in0=xs[j], scalar=gsb[:, 2 * w + 1:2 * w + 2], in1=s_,
                     op0=ALU.mult, op1=ALU.mult, reads=[XS[j], GSB, S_], writes=[TB[j]])
                P.op("dve", "tensor_tensor", out=ta[j], in0=ta[j], in1=tb[j], op=ALU.add, reads=[TA[j], TB[j]], writes=[TA[j]])
                P.op("dve", "tensor_tensor", out=dst[:, tsl], in0=ta[j], in1=rs[j], op=ALU.mult,
                     reads=[TA[j], RSB[j]], writes=[DST])
                kk += 1
        SBK = 2048
        acc = [A.alloc([2, SBK], F32, parts=64) for _ in range(2)]; ACC = [P.buf() for _ in range(2)]
        rden = A.alloc([SBK], F32, parts=64); RDEN = P.buf("rden")
        od = [A.alloc([SBK], BF16, parts=64) for _ in range(2)]; OD = [P.buf() for _ in range(2)]
        NP3 = 3
        p_sb = [A.alloc([256], BF16) for _ in range(NP3)]; PSB = [P.buf() for _ in range(NP3)]
        blocks = []
        for sbi in range(S // SBK):
            for r in (1, 4, 16):
                for n in range(sbi * 16 // r, (sbi + 1) * 16 // r):
                    for rho in range(r):
                        blocks.append((sbi, r, n, rho))
        NBL = len(blocks)

        def cols(r, n, rho):
            start = n * 128 * r + rho
            return slice(start, start + 127 * r + 1, r)

        def d1(m):
            sbi, r, n, rho = blocks[m]
            bz = m % 2
            mms = []
            if n > 0:
                mms.append(dict(out=banks[bz][:, 0:128], lhsT=kd_sb[:, cols(r, n - 1, rho)], rhs=qd_sb[:, cols(r, n, rho)],
                                start=True, stop=True))
            mms.append(dict(out=banks[bz][:, 128:256], lhsT=kd_sb[:, cols(r, n, rho)], rhs=qd_sb[:, cols(r, n, rho)],
                            start=True, stop=True))
            P.mm(mms, reads=[KD, QD], writes=[BK[bz]])

        def d2(m):
            sbi, r, n, rho = blocks[m]
            bz = m % 2
            lo = 0 if n > 0 else 128
            P.op("act", "activation", out=p_sb[m % NP3][:, lo:256], in_=banks[bz][:, lo:256], func=AF.Exp,
                 reads=[BK[bz]], writes=[PSB[m % NP3]])

        def d3(m):
            sbi, r, n, rho = blocks[m]
            lo = 0 if n > 0 else 128
            P.op("pool", "tensor_tensor", out=p_sb[m % NP3][:, lo:256], in0=p_sb[m % NP3][:, lo:256], in1=md[:, lo:256],
                 op=ALU.mult, reads=[PSB[m % NP3], MD], writes=[PSB[m % NP3]])

        def d4(m):
            sbi, r, n, rho = blocks[m]
            bn = 2 + m % 2
            p = p_sb[m % NP3]
            v = vds[r]
            mms = []
            vown = v[:, n, rho * HD:(rho + 1) * HD]
            if n > 0:
                vprev = v[:, n - 1, rho * HD:(rho + 1) * HD]
                mms.append(dict(out=banks[bn][0:64, 0:128], lhsT=vprev, rhs=p[:, 0:128], start=True, stop=False))
                mms.append(dict(out=banks[bn][0:64, 0:128], lhsT=vown, rhs=p[:, 128:256], start=False, stop=True))
                mms.append(dict(out=banks[bn][0:64, 128:256], lhsT=ones_bf[:, 0:64], rhs=p[:, 0:128], start=True, stop=False))
                mms.append(dict(out=banks[bn][0:64, 128:256], lhsT=ones_bf[:, 0:64], rhs=p[:, 128:256], start=False, stop=True))
            else:
                mms.append(dict(out=banks[bn][0:64, 0:128], lhsT=vown, rhs=p[:, 128:256], start=True, stop=True))
                mms.append(dict(out=banks[bn][0:64, 128:256], lhsT=ones_bf[:, 0:64], rhs=p[:, 128:256], start=True, stop=True))
            P.mm(mms, reads=[VDS[r], ONES, PSB[m % NP3]], writes=[BK[bn]])

        def d5(m):
            sbi, r, n, rho = blocks[m]
            bn = 2 + m % 2
            a = acc[sbi % 2]; AC = ACC[sbi % 2]
            loc = (n * 128 * r + rho) - sbi * SBK
            dst = a[:, :, loc:loc + 127 * r + 1:r]
            src = banks[bn][0:64, 0:256].rearrange("p (a b) -> p a b", a=2)
            if r == 1:
                P.op("dve", "tensor_copy", out=dst, in_=src, reads=[BK[bn]], writes=[AC])
            else:
                P.op("dve", "tensor_tensor", out=dst, in0=src, in1=dst, op=ALU.add, reads=[BK[bn], AC], writes=[AC])
            last = (r == 16 and n == (sbi + 1) - 1 and rho == 15)
            if last:
                P.op("dve", "reciprocal", out=rden, in_=a[:, 1, :], reads=[AC], writes=[RDEN])
                o = od[sbi % 2]; O = OD[sbi % 2]
                P.op("dve", "tensor_tensor", out=o, in0=a[:, 0, :], in1=rden, op=ALU.mult, reads=[AC, RDEN], writes=[O])
                P.dma("sp", yT[64:128, sbi * SBK:(sbi + 1) * SBK], o, st, reads=[O])

        dst_ = [d1, d2, d3, d4, d5]
        for step in range(NBL + 4):
            for k_, fn in enumerate(dst_):
                m = step - k_
                if 0 <= m < NBL:
                    fn(m)

    if "hgrn" in phases:
        barrier(P)
        A.reset()
        TT = 512
        NCH = TT // 64
        lb = A.alloc([4], F32); LB = P.buf("lb")
        P.dma("sp", lb[:, 0:2], lbl, cl, writes=[LB])
        if layer == 0:
            P.op("dve", "memset", ap=lb[:, 2:3], constant=0.0, reads=[LB], writes=[LB])
            P.op("dve", "memset", ap=lb[:, 3:4], constant=1.0, reads=[LB], writes=[LB])
        else:
            P.op("dve", "tensor_tensor", out=lb[:, 2:3], in0=lb[:, 0:1], in1=lb[:, 1:2], op=ALU.subtract, reads=[LB], writes=[LB])
            P.op("act", "activation", out=lb[:, 2:3], in_=lb[:, 2:3], func=AF.Exp, reads=[LB], writes=[LB])
            P.op("dve", "tensor_scalar_add", out=lb[:, 2:3], in0=lb[:, 2:3], scalar1=1.0, reads=[LB], writes=[LB])
            P.op("dve", "reciprocal", out=lb[:, 2:3], in_=lb[:, 2:3], reads=[LB], writes=[LB])
            P.op("dve", "tensor_scalar_min", out=lb[:, 2:3], in0=lb[:, 2:3], scalar1=1.0 - EPS, reads=[LB], writes=[LB])
            P.op("dve", "tensor_scalar", out=lb[:, 3:4], in0=lb[:, 2:3], scalar1=-1.0, scalar2=1.0, op0=ALU.mult, op1=ALU.add,
                 reads=[LB], writes=[LB])
        hgn_sb = A.alloc([1], F32); HGN = P.buf("hgn")
        P.dma("sp", hgn_sb, hgn, cl, writes=[HGN])
        rmask = A.alloc([TT], F32); RMASK = P.buf("rmask")
        P.dma("sp", rmask, rmask_d, cl, writes=[RMASK])
        mh = A.alloc([64], BF16, parts=64); MH = P.buf("mh")
        P.dma("sp", mh, mh_d, cl, writes=[MH])
        ident = A.alloc([128], BF16); IDENT = P.buf("ident")
        P.dma("sp", ident, ident_d, cl, writes=[IDENT])
        s_f = A.alloc([128], F32); SF = P.buf("s_f")
        s_b = [A.alloc([128], BF16) for _ in range(2)]; SBF = [P.buf() for _ in range(2)]
        P.op("dve", "memset", ap=s_f, constant=0.0, writes=[SF])
        P.op("dve", "memset", ap=s_b[0], constant=0.0, writes=[SBF[0]])

        def al(n, dt, parts=128, shape=None):
            return [A.alloc(shape or [TT], dt, parts=parts) for _ in range(n)], [P.buf() for _ in range(n)]
        zq, ZQ = al(2, F32); zf, ZF = al(2, F32); zg, ZG = al(2, F32)
        vt, VT = al(2, BF16, parts=64, shape=[NCH, 128])
        t1, T1 = al(2, F32); t2, T2 = al(2, F32); t3, T3 = al(2, F32); t4, T4 = al(2, F32); t5, T5 = al(2, F32)
        Gt, GT = al(2, F32)
        qs, QS = al(2, F32)
        kkf, KKF = al(2, F32)
        qt_, QTL = al(2, BF16); qg, QG = al(2, BF16); kh, KH = al(2, BF16)
        kt4 = []; KT4 = []
        for _i in range(4):
            a_, b_ = al(2, BF16); kt4.append(a_); KT4.append(b_)
        dec, DEC = al(2, F32, shape=[NCH])
        kht, KHT = al(2, BF16, parts=64, shape=[NCH, 128])
        at, AT = al(3, BF16, parts=64, shape=[64])
        osq, OSQ = al(2, BF16)
        orr, ORR = al(2, F32)
        gate, GATE = al(2, F32)
        yo, YO = al(2, BF16)
        sidx = 0
        for t in range(S // TT):
            j = t % 2
            tsl = slice(t * TT, (t + 1) * TT)
            P.dma("sp", zq[j], hg[0, :, tsl], ld, writes=[ZQ[j]])
            P.dma("sp", zf[j], hg[1, :, tsl], ld, writes=[ZF[j]])
            P.dma("sp", zg[j], hg[2, :, tsl], ld, writes=[ZG[j]])
            P.dma("sp", vt[j], ih[tsl, :].rearrange("(n p) d -> p n d", p=64), ld, writes=[VT[j]])
            P.op("act", "activation", out=t1[j], in_=zf[j], func=AF.Exp, scale=-1.0, reads=[ZF[j]], writes=[T1[j]])
            P.op("pool", "tensor_scalar_add", out=t1[j], in0=t1[j], scalar1=1.0, reads=[T1[j]], writes=[T1[j]])
            P.op("dve", "reciprocal", out=t1[j], in_=t1[j], reads=[T1[j]], writes=[T1[j]])
            P.op("dve", "tensor_scalar", out=t2[j], in0=t1[j], scalar1=lb[:, 3:4], scalar2=lb[:, 2:3], op0=ALU.mult, op1=ALU.add,
                 reads=[T1[j], LB], writes=[T2[j]])
            P.op("pool", "tensor_scalar", out=kkf[j], in0=t2[j], scalar1=-1.0, scalar2=1.0, op0=ALU.mult, op1=ALU.add,
                 reads=[T2[j]], writes=[KKF[j]])
            P.op("act", "activation", out=t3[j], in_=t2[j], func=AF.Ln, reads=[T2[j]], writes=[T3[j]])
            P.op("dve", "tensor_tensor_scan", out=Gt[j], data0=rmask, data1=t3[j], initial=0.0, op0=ALU.mult, op1=ALU.add,
                 reads=[RMASK, T3[j]], writes=[GT[j]])
            G3 = Gt[j].rearrange("p (c t) -> p c t", t=64)
            P.op("act", "activation", out=t4[j], in_=zq[j], func=AF.Exp, scale=-1.0, reads=[ZQ[j]], writes=[T4[j]])
            P.op("pool", "tensor_scalar_add", out=t4[j], in0=t4[j], scalar1=1.0, reads=[T4[j]], writes=[T4[j]])
            P.op("dve", "reciprocal", out=t4[j], in_=t4[j], reads=[T4[j]], writes=[T4[j]])
            P.op("pool", "tensor_tensor", out=qs[j], in0=zq[j], in1=t4[j], op=ALU.mult, reads=[ZQ[j], T4[j]], writes=[QS[j]])
            glast = G3[:, :, 63:64].to_broadcast([128, NCH, 64])
            t5v = t5[j].rearrange("p (c t) -> p c t", t=64)
            G16 = Gt[j].rearrange("p (n t) -> p n t", t=16)
            t5v16 = t5[j].rearrange("p (n t) -> p n t", t=16)
            P.op("dve", "tensor_tensor", out=t5v16, in0=G16, in1=G16[:, :, 0:1].to_broadcast([128, TT // 16, 16]),
                 op=ALU.subtract, reads=[GT[j]], writes=[T5[j]])
            P.op("act", "activation", out=t1[j], in_=t5[j], func=AF.Exp, reads=[T5[j]], writes=[T1[j]])
            P.op("dve", "tensor_tensor", out=qt_[j], in0=qs[j], in1=t1[j], op=ALU.mult, reads=[QS[j], T1[j]], writes=[QTL[j]])
            for i4 in range(4):
                gb = G3[:, :, 16 * i4:16 * i4 + 1].to_broadcast([128, NCH, 64])
                P.op("dve", "tensor_tensor", out=t5v, in0=gb, in1=G3, op=ALU.subtract, reads=[GT[j]], writes=[T5[j]])
                P.op("dve", "tensor_scalar_min", out=t5[j], in0=t5[j], scalar1=50.0, reads=[T5[j]], writes=[T5[j]])
                P.op("act", "activation", out=t2[j], in_=t5[j], func=AF.Exp, reads=[T5[j]], writes=[T2[j]])
                P.op("pool", "tensor_tensor", out=kt4[i4][j], in0=kkf[j], in1=t2[j], op=ALU.mult,
                     reads=[KKF[j], T2[j]], writes=[KT4[i4][j]])
            P.op("act", "activation", out=t3[j], in_=Gt[j], func=AF.Exp, reads=[GT[j]], writes=[T3[j]])
            P.op("dve", "tensor_tensor", out=qg[j], in0=qs[j], in1=t3[j], op=ALU.mult, reads=[QS[j], T3[j]], writes=[QG[j]])
            P.op("dve", "tensor_tensor", out=t5v, in0=glast, in1=G3, op=ALU.subtract, reads=[GT[j], T1[j], T2[j]], writes=[T5[j]])
            P.op("act", "activation", out=t1[j], in_=t5[j], func=AF.Exp, reads=[T5[j], QTL[j]], writes=[T1[j]])
            P.op("pool", "tensor_tensor", out=kh[j], in0=kkf[j], in1=t1[j], op=ALU.mult, reads=[KKF[j], T1[j]], writes=[KH[j]])
            P.op("act", "activation", out=dec[j], in_=G3[:, :, 63], func=AF.Exp, reads=[GT[j]], writes=[DEC[j]])
            P.op("act", "activation", out=t4[j], in_=zg[j], func=AF.Exp, scale=-1.0, reads=[ZG[j], QS[j]], writes=[T4[j]])
            P.op("pool", "tensor_scalar_add", out=t4[j], in0=t4[j], scalar1=1.0, reads=[T4[j]], writes=[T4[j]])
            P.op("dve", "reciprocal", out=t4[j], in_=t4[j], reads=[T4[j]], writes=[T4[j]])
            P.op("pool", "tensor_tensor", out=gate[j], in0=zg[j], in1=t4[j], op=ALU.mult, reads=[ZG[j], T4[j]], writes=[GATE[j]])
            for c in range(NCH):
                P.op("pe", "transpose", out=banks[6].bitcast(BF16)[0:64, c * 128:(c + 1) * 128],
                     in_=kh[j][:, c * 64:(c + 1) * 64], identity=ident, reads=[KH[j], IDENT], writes=[BK[6]])
            P.op("act", "copy", out=kht[j], in_=banks[6].bitcast(BF16)[0:64, 0:NCH * 128].rearrange("p (c d) -> p c d", c=NCH),
                 reads=[BK[6]], writes=[KHT[j]])
            bo = 4 + t % 2
            for c in range(NCH):
                csl = slice(c * 64, (c + 1) * 64)
                ba = sidx % 2
                a3 = sidx % 3
                P.mm([dict(out=banks[ba][0:64, 16 * i4:16 * i4 + 16], lhsT=kt4[i4][j][:, csl],
                           rhs=qt_[j][:, c * 64 + 16 * i4:c * 64 + 16 * i4 + 16], start=True, stop=True) for i4 in range(4)],
                     reads=[KT4[0][j], KT4[1][j], KT4[2][j], KT4[3][j], QTL[j]], writes=[BK[ba]])
                P.op("dve", "tensor_tensor", out=at[a3], in0=banks[ba][0:64, 0:64], in1=mh, op=ALU.mult,
                     reads=[BK[ba], MH], writes=[AT[a3]])
                scur = s_b[sidx % 2]; SCUR = SBF[sidx % 2]
                snext = s_b[(sidx + 1) % 2]; SNEXT = SBF[(sidx + 1) % 2]
                P.mm([dict(out=banks[bo][:, csl], lhsT=vt[j][:, c, :], rhs=at[a3], start=True, stop=False),
                      dict(out=banks[bo][:, csl], lhsT=scur, rhs=qg[j][:, csl], start=False, stop=True)],
                     reads=[VT[j], AT[a3], SCUR, QG[j]], writes=[BK[bo]])
                bs = 2 + sidx % 2
                P.mm([dict(out=banks[bs][:, 0:128], lhsT=kht[j][:, c, :], rhs=vt[j][:, c, :], start=True, stop=True)],
                     reads=[KHT[j], VT[j]], writes=[BK[bs]])
                P.op("dve", "scalar_tensor_tensor", out=snext, in0=s_f, scalar=dec[j][:, c:c + 1], in1=banks[bs][:, 0:128],
                     op0=ALU.mult, op1=ALU.add, reads=[SF, DEC[j], BK[bs]], writes=[SNEXT])
                P.op("dve", "scalar_tensor_tensor", out=s_f, in0=s_f, scalar=dec[j][:, c:c + 1], in1=banks[bs][:, 0:128],
                     op0=ALU.mult, op1=ALU.add, reads=[SF, DEC[j], BK[bs]], writes=[SF])
                sidx += 1
            P.op("act", "activation", out=osq[j], in_=banks[bo][:], func=AF.Square, reads=[BK[bo]], writes=[OSQ[j]])
            P.mm([dict(out=banks[7][:], lhsT=ones_bf, rhs=osq[j], start=True, stop=True)], reads=[ONES, OSQ[j]], writes=[BK[7]])
            P.op("act", "activation", out=orr[j], in_=banks[7][:], func=AF.Sqrt, bias=EPS, scale=1.0 / 128,
                 reads=[BK[7]], writes=[ORR[j]])
            P.op("dve", "reciprocal", out=orr[j], in_=orr[j], reads=[ORR[j]], writes=[ORR[j]])
            P.op("dve", "scalar_tensor_tensor", out=orr[j], in0=orr[j], scalar=hgn_sb[:, 0:1], in1=gate[j],
                 op0=ALU.mult, op1=ALU.mult, reads=[ORR[j], HGN, GATE[j]], writes=[ORR[j]])
            P.op("dve", "tensor_tensor", out=yo[j], in0=banks[bo][:], in1=orr[j], op=ALU.mult, reads=[BK[bo], ORR[j]], writes=[YO[j]])
            P.dma("sp", yT[128:256, tsl], yo[j], st, reads=[YO[j]])
    P.build()
    return nc, P


BF = ml_dtypes.bfloat16
S = 16384; HD = 64
def mix_consts():
    half = HD // 2
    inv_freq = (10000.0 ** (-np.arange(half, dtype=np.float32) * 2.0 / HD)).astype(np.float32)
    ang = (np.arange(S, dtype=np.float32)[:, None] * inv_freq[None, :]).astype(np.float32)
    cos = np.cos(ang).astype(np.float32).T; sin = np.sin(ang).astype(np.float32).T
    cosT = np.ascontiguousarray(np.concatenate([cos, cos], 0))
    sinT = np.ascontiguousarray(np.concatenate([-sin, sin], 0))
    k = np.arange(128)[:, None]; q = np.arange(512)[None, :]
    msb = np.stack([(q > r * 128 + k) for r in range(4)]).astype(np.float32).astype(BF)
    jj = np.arange(128)[:, None]; kk = np.arange(128)[None, :]
    lneg = (-(jj >= kk).astype(np.float32)).astype(BF)
    a = np.arange(128)[None, :]
    mdil = np.concatenate([(k >= a), (k <= a)], 1).astype(np.float32).astype(BF)
    s_ = np.arange(64)[:, None]; t_ = np.arange(64)[None, :]
    mh = (s_ <= t_).astype(np.float32).astype(BF)
    ident = np.eye(128, dtype=np.float32).astype(BF)
    rmask = np.ones((128, 512), np.float32); rmask[:, ::64] = 0.0
    return dict(cosT=cosT, sinT=sinT, msb=msb, lneg=lneg, mdil=mdil, mh=mh, ident=ident, rmask=rmask)

def mix_inputs(pfb, pff, ptb, i, l, q_norm_g, k_norm_g, hgrn_norm_g, hgrn_lb_logits, consts):
    h64 = slice(i * 64, (i + 1) * 64); h128 = slice(i * 128, (i + 1) * 128)
    sw = np.concatenate([np.arange(32, 64), np.arange(0, 32)])
    gq = q_norm_g[l]; gk = k_norm_g[l]
    m = dict(consts)
    m["qk"] = np.ascontiguousarray(np.stack([pfb[0:256][h64], pfb[256:512][h64]]))
    m["va"] = np.ascontiguousarray(ptb[:, 0:256][:, h64])
    m["qkd"] = np.ascontiguousarray(np.stack([pff[0:256][h64], pff[256:512][h64]]))
    m["vd"] = np.ascontiguousarray(ptb[:, 256:512][:, h64])
    m["hg"] = np.ascontiguousarray(np.stack([pff[512:1024][h128], pff[1024:1536][h128], pff[1536:2048][h128]]))
    m["ih"] = np.ascontiguousarray(ptb[:, 512:1024][:, h128])
    m["gn"] = np.ascontiguousarray(np.stack([gq, gq[sw], gk, gk[sw]], 1).astype(np.float32))
    m["hgn"] = np.ascontiguousarray(hgrn_norm_g[l].reshape(128, 1).astype(np.float32))
    m["lbl"] = np.ascontiguousarray(hgrn_lb_logits[:, h128].T.astype(np.float32))
    return m


DEPTH = 2
_WNAMES = ("ffn1_w_gate", "ffn1_w_up", "ffn1_w_down", "ffn2_w_gate", "ffn2_w_up", "ffn2_w_down")


def _tok_maps(stages, layers_mod, xT_list, inp, yT_list=None):
    maps = []
    for core in range(8):
        b = core // 4
        m = {"xT": xT_list[core], "cT": np.ascontiguousarray(inp["c"][b].reshape(8, 128).T)}
        for l in layers_mod:
            m[f"bmod{l}"] = np.ascontiguousarray(inp["b_mod"][l].reshape(72, 128).T)
            m[f"wmod{l}"] = np.ascontiguousarray(inp["w_mod"][l])
        for st_ in stages:
            l = st_[1]
            if st_[0] == "ffn":
                w = st_[2]
                m[f"wg{l}_{w}"] = np.ascontiguousarray(inp[f"ffn{w}_w_gate"][l])
                m[f"wu{l}_{w}"] = np.ascontiguousarray(inp[f"ffn{w}_w_up"][l])
                m[f"wd{l}_{w}"] = np.ascontiguousarray(inp[f"ffn{w}_w_down"][l])
            elif st_[0] == "proj":
                m[f"win{l}"] = np.ascontiguousarray(inp["w_in"][l])
            elif st_[0] == "wout":
                m[f"wout{l}"] = np.ascontiguousarray(inp["w_out"][l])
                m["yT"] = yT_list[core]
        maps.append(m)
    return maps


def _run_mixer(l, resA, inp, consts):
    nc, _ = build_mix(l)
    maps = []
    for core in range(8):
        b, i = core // 4, core % 4
        rs = resA[4 * b:4 * b + 4]
        pfb = np.concatenate([r["pfb"] for r in rs], axis=1)
        pff = np.concatenate([r["pff"] for r in rs], axis=1)
        ptb = np.concatenate([r["ptb"] for r in rs], axis=0)
        maps.append(mix_inputs(pfb, pff, ptb, i, l, inp["q_norm_g"], inp["k_norm_g"], inp["hgrn_norm_g"],
                               inp["hgrn_lb_logits"], consts))
    res = run_bass_kernel_spmd(nc, maps, core_ids=list(range(8))).results
    yT_list = []
    for b in range(2):
        ys = [res[4 * b + i]["yT"] for i in range(4)]
        yfull = np.concatenate([y[0:64] for y in ys] + [y[64:128] for y in ys] + [y[128:256] for y in ys], axis=0)
        for j in range(4):
            yT_list.append(np.ascontiguousarray(yfull[:, j * NTOK:(j + 1) * NTOK]))
    return yT_list


def kernel(**inputs):
    inp = {k: np.asarray(v) for k, v in inputs.items()}
    x = inp["x"].astype(np.float32, copy=False)
    B, S_, D_ = x.shape
    xf = x.reshape(B * S_, D_)
    xT_list = [np.ascontiguousarray(xf[core * NTOK:(core + 1) * NTOK].T) for core in range(8)]
    consts = mix_consts()
    yT_list = None
    for l in range(DEPTH):
        if l == 0:
            stages = [("ffn", 0, 1), ("proj", 0)]
            mods = {0: [0, 1, 2, 3, 4]}
        else:
            stages = [("wout", l - 1), ("ffn", l - 1, 2), ("ffn", l, 1), ("proj", l)]
            mods = {l - 1: [5, 6, 7, 8], l: [0, 1, 2, 3, 4]}
        nc, _ = build_tok(stages, mods)
        resA = run_bass_kernel_spmd(nc, _tok_maps(stages, mods, xT_list, inp, yT_list), core_ids=list(range(8))).results
        xT_list = [r["xo"] for r in resA]
        yT_list = _run_mixer(l, resA, inp, consts)
    l = DEPTH - 1
    stages = [("wout", l), ("ffn", l, 2)]
    mods = {l: [5, 6, 7, 8]}
    nc, _ = build_tok(stages, mods)
    resC = run_bass_kernel_spmd(nc, _tok_maps(stages, mods, xT_list, inp, yT_list), core_ids=list(range(8))).results
    out = np.concatenate([r["xo"].T for r in resC], axis=0).reshape(B, S_, D_)
    return np.ascontiguousarray(out.astype(np.float32, copy=False))
```
